# Optimizing a Trainium2 kernel written in Bass

```python
import math
import jax, jax.numpy as jnp
from jax import lax
import numpy as np

D_MODEL = 1024
BATCH = 2
SEQ = 8192
DEPTH = 2

GRID_W = 64
CTX_LEN = 256
F_GROUPS = 4
F_GROUP_DIM = 128
F_WIDTH = F_GROUPS * F_GROUP_DIM
DN_HEADS = 4
DN_HEAD_DIM = 128
DN_WIDTH = DN_HEADS * DN_HEAD_DIM
DN_CONV = 5
DN_CHUNK = 64
DA_HEADS = 4
DA_QK_DIM = 64
DA_V_DIM = 2 * DA_QK_DIM
DA_QK_WIDTH = DA_HEADS * 2 * DA_QK_DIM
DA_WIDTH = DA_HEADS * DA_V_DIM
Q_BLOCK = 128
ROPE_THETA = 10000.0
D_FF = 4 * D_MODEL
N_BRANCH = 3
NORM_EPS = 1e-6
IN_SPLITS = (F_WIDTH, 3 * DN_WIDTH, DN_WIDTH, 4 * DN_HEADS, DA_QK_WIDTH, DA_QK_WIDTH, DA_WIDTH, N_BRANCH * D_MODEL)
D_IN = sum(IN_SPLITS)

kernel_name = 'hybrid_fourier_deltanet_diffattn_dit'


def rms_norm(x, gain):
    xf = x.astype(jnp.float32)
    y = xf * lax.rsqrt(jnp.mean(xf * xf, axis=-1, keepdims=True) + NORM_EPS)
    return (y * gain.astype(jnp.float32)).astype(x.dtype)


def l2_normalize(x):
    return x * lax.rsqrt(jnp.sum(x * x, axis=-1, keepdims=True) + NORM_EPS)


def split_columns(p):
    offsets = []
    acc = 0
    for n in IN_SPLITS[:-1]:
        acc += n
        offsets.append(acc)
    return jnp.split(p, offsets, axis=-1)


def axial_rope_tables(rows):
    t = jnp.arange(rows * GRID_W)
    row = (t // GRID_W).astype(jnp.float32)
    col = (t % GRID_W).astype(jnp.float32)
    n_freq = DA_QK_DIM // 4
    inv_freq = ROPE_THETA ** (-jnp.arange(n_freq, dtype=jnp.float32) / n_freq)
    ang_r = row[:, None] * inv_freq[None, :]
    ang_c = col[:, None] * inv_freq[None, :]
    return (jnp.cos(ang_r), jnp.sin(ang_r), jnp.cos(ang_c), jnp.sin(ang_c))


def rotate_pairs(x, cos, sin):
    x1, x2 = jnp.split(x, 2, axis=-1)
    return jnp.concatenate([x1 * cos - x2 * sin, x2 * cos + x1 * sin], axis=-1)


def apply_axial_rope(x, tabs):
    cos_r, sin_r, cos_c, sin_c = (tab[:, None, None, :].astype(x.dtype) for tab in tabs)
    x_r, x_c = jnp.split(x, 2, axis=-1)
    return jnp.concatenate([rotate_pairs(x_r, cos_r, sin_r), rotate_pairs(x_c, cos_c, sin_c)], axis=-1)


def centred_depthwise_conv(u, w):
    pad = w.shape[0] // 2
    return lax.conv_general_dilated(u, w[:, None, :].astype(u.dtype), (1,), [(pad, pad)],
                                    dimension_numbers=('NWC', 'WIO', 'NWC'),
                                    feature_group_count=u.shape[-1])


def fourier_mix(u):
    b, t, _ = u.shape
    uf = u.astype(jnp.float32).reshape(b, t, F_GROUPS, F_GROUP_DIM)
    y = jnp.fft.fft2(uf, axes=(1, 3), norm='ortho').real
    return y.reshape(b, t, F_WIDTH).astype(u.dtype)


def deltanet_inputs(qkv, ab, conv_w, a_log, dt_bias):
    b, t, _ = qkv.shape
    qkv = jax.nn.silu(centred_depthwise_conv(qkv, conv_w)).astype(jnp.float32)
    q, k, v = jnp.split(qkv, 3, axis=-1)
    q = l2_normalize(q.reshape(b, t, DN_HEADS, DN_HEAD_DIM)) * (DN_HEAD_DIM ** -0.5)
    k = l2_normalize(k.reshape(b, t, DN_HEADS, DN_HEAD_DIM))
    v = v.reshape(b, t, DN_HEADS, DN_HEAD_DIM)
    ab = ab.astype(jnp.float32).reshape(b, t, 4, DN_HEADS)
    beta = jax.nn.sigmoid(ab[:, :, 0:2])
    g = -jnp.exp(a_log.astype(jnp.float32)) * jax.nn.softplus(ab[:, :, 2:4] + dt_bias.astype(jnp.float32))
    return (q, k, v, g, beta)


def gated_delta_chunked(q, k, v, g, beta, state0, with_out):
    b, t, h, _ = k.shape
    dv = v.shape[-1]
    n = t // DN_CHUNK

    def to_chunks(z):
        return z.reshape(b, n, DN_CHUNK, h, -1).transpose(1, 0, 3, 2, 4)

    kc, vc = to_chunks(k), to_chunks(v)
    bc = to_chunks(beta[..., None])
    gcum = jnp.cumsum(to_chunks(g[..., None])[..., 0], axis=-1)
    idx = jnp.arange(DN_CHUNK)
    lower = idx[:, None] >= idx[None, :]
    strict = idx[:, None] > idx[None, :]
    decay = jnp.exp(jnp.where(lower, gcum[..., :, None] - gcum[..., None, :], -jnp.inf))
    kb = kc * bc
    lmat = jnp.where(strict, jnp.einsum('nbhid,nbhjd->nbhij', kb, kc) * decay, 0.0)
    eye = jnp.broadcast_to(jnp.eye(DN_CHUNK, dtype=jnp.float32), lmat.shape)
    tmat = lax.linalg.triangular_solve(eye + lmat, eye, left_side=True, lower=True, unit_diagonal=True)
    u = tmat @ (vc * bc)
    w = tmat @ (kb * jnp.exp(gcum)[..., None])
    g_last = gcum[..., -1]
    k_to_end = kc * jnp.exp(g_last[..., None] - gcum)[..., None]

    if with_out:
        qc = to_chunks(q)
        qk = jnp.einsum('nbhid,nbhjd->nbhij', qc, kc) * decay
        q_dec = qc * jnp.exp(gcum)[..., None]

        def step(s, xs):
            u_i, w_i, ke_i, gl_i, qd_i, qk_i = xs
            v_new = u_i - w_i @ s
            out = qd_i @ s + qk_i @ v_new
            s = s * jnp.exp(gl_i)[..., None, None] + jnp.einsum('bhcd,bhce->bhde', ke_i, v_new)
            return s, out

        s, outs = lax.scan(step, state0, (u, w, k_to_end, g_last, q_dec, qk))
        return s, outs.transpose(1, 0, 3, 2, 4).reshape(b, t, h, dv)

    def step_state(s, xs):
        u_i, w_i, ke_i, gl_i = xs
        v_new = u_i - w_i @ s
        s = s * jnp.exp(gl_i)[..., None, None] + jnp.einsum('bhcd,bhce->bhde', ke_i, v_new)
        return s, None

    s, _ = lax.scan(step_state, state0, (u, w, k_to_end, g_last))
    return s, None


def flip_time(z):
    return jnp.flip(z, axis=1)


def bidirectional_deltanet(lat, ctx, need_ctx):
    q, k, v, g, beta = lat
    qc, kc, vc, gc, bc = ctx
    s0 = jnp.zeros((q.shape[0], DN_HEADS, DN_HEAD_DIM, DN_HEAD_DIM), jnp.float32)
    s_ctx_f, o_ctx_f = gated_delta_chunked(qc, kc, vc, gc[:, :, 0], bc[:, :, 0], s0, need_ctx)
    _, o_lat_f = gated_delta_chunked(q, k, v, g[:, :, 0], beta[:, :, 0], s_ctx_f, True)
    s_ctx_b, o_ctx_b = gated_delta_chunked(flip_time(qc), flip_time(kc), flip_time(vc), flip_time(gc[:, :, 1]),
                                           flip_time(bc[:, :, 1]), s0, need_ctx)
    _, o_lat_b = gated_delta_chunked(flip_time(q), flip_time(k), flip_time(v), flip_time(g[:, :, 1]),
                                     flip_time(beta[:, :, 1]), s_ctx_b, True)
    o_lat = o_lat_f + flip_time(o_lat_b)
    o_ctx = (o_ctx_f + flip_time(o_ctx_b)) if need_ctx else None
    return o_lat, o_ctx


def deltanet_output(o, z, gain, dtype):
    b, t = o.shape[:2]
    z = z.astype(jnp.float32).reshape(b, t, DN_HEADS, DN_HEAD_DIM)
    return (rms_norm(o, gain) * jax.nn.silu(z)).astype(dtype).reshape(b, t, DN_WIDTH)


def diff_attend(q, k, v, lam):
    s = jnp.einsum('bqhmd,bkhmd->bhmqk', q, k, preferred_element_type=jnp.float32) * (DA_QK_DIM ** -0.5)
    p = jax.nn.softmax(s, axis=-1)
    a = p[:, :, 0] - lam * p[:, :, 1]
    return jnp.einsum('bhqk,bkhe->bqhe', a.astype(v.dtype), v)


def differential_attention_latent(q, k_lat, v_lat, k_ctx, v_ctx, lam):
    b, t = q.shape[:2]
    k_all = jnp.concatenate([k_ctx, k_lat], axis=1)
    v_all = jnp.concatenate([v_ctx, v_lat], axis=1)
    n_blk = t // Q_BLOCK
    q_blocks = jnp.swapaxes(q.reshape(b, n_blk, Q_BLOCK, DA_HEADS, 2, DA_QK_DIM), 0, 1)
    o = lax.map(lambda qb: diff_attend(qb, k_all, v_all, lam), q_blocks)
    return jnp.swapaxes(o, 0, 1).reshape(b, t, DA_HEADS, DA_V_DIM)


def diff_output(o, gain, lam_init):
    b, t = o.shape[:2]
    return (rms_norm(o, gain) * (1.0 - lam_init)).reshape(b, t, DA_WIDTH)


def gated_merge(y_f, y_dn, y_da, gates):
    g_f, g_dn, g_da = jnp.split(jax.nn.sigmoid(gates.astype(jnp.float32)).astype(gates.dtype), N_BRANCH, axis=-1)
    return g_f * y_f + g_dn * y_dn + g_da * y_da


def hybrid_mixer(h, hc, w_in, conv_w, a_log, dt_bias, dn_gain, lam_vecs, da_gain, w_f, w_dn, w_da, w_o,
                 tabs, lam_init, need_ctx):
    b, t, _ = h.shape
    tc = hc.shape[1]
    u_f, dn_qkv, dn_z, dn_ab, da_q, da_k, da_v, gates = split_columns(h @ w_in)
    u_fc, dn_qkvc, dn_zc, dn_abc, da_qc, da_kc, da_vc, gatesc = split_columns(hc @ w_in)

    y_f = fourier_mix(u_f) @ w_f
    o_dn, o_dnc = bidirectional_deltanet(deltanet_inputs(dn_qkv, dn_ab, conv_w, a_log, dt_bias),
                                         deltanet_inputs(dn_qkvc, dn_abc, conv_w, a_log, dt_bias), need_ctx)
    y_dn = deltanet_output(o_dn, dn_z, dn_gain, h.dtype) @ w_dn
    lq1, lk1, lq2, lk2 = lam_vecs.astype(jnp.float32)
    lam = jnp.exp(jnp.sum(lq1 * lk1)) - jnp.exp(jnp.sum(lq2 * lk2)) + lam_init
    q_l = apply_axial_rope(da_q.reshape(b, t, DA_HEADS, 2, DA_QK_DIM), tabs)
    k_l = apply_axial_rope(da_k.reshape(b, t, DA_HEADS, 2, DA_QK_DIM), tabs)
    v_l = da_v.reshape(b, t, DA_HEADS, DA_V_DIM)
    k_c = da_kc.reshape(b, tc, DA_HEADS, 2, DA_QK_DIM)
    v_c = da_vc.reshape(b, tc, DA_HEADS, DA_V_DIM)
    o_da = differential_attention_latent(q_l, k_l, v_l, k_c, v_c, lam)
    y_da = diff_output(o_da, da_gain, lam_init) @ w_da
    y = gated_merge(y_f, y_dn, y_da, gates) @ w_o
    if not need_ctx:
        return y, None
    y_fc = fourier_mix(u_fc) @ w_f
    y_dnc = deltanet_output(o_dnc, dn_zc, dn_gain, hc.dtype) @ w_dn
    o_dac = diff_attend(da_qc.reshape(b, tc, DA_HEADS, 2, DA_QK_DIM), k_c, v_c, lam)
    y_dac = diff_output(o_dac, da_gain, lam_init) @ w_da
    yc = gated_merge(y_fc, y_dnc, y_dac, gatesc) @ w_o
    return y, yc


def squared_relu_mlp(h, w1, w2):
    a = jax.nn.relu(h @ w1)
    return (a * a) @ w2


def setup_inputs(seed: int = 0) -> dict:
    key = jax.random.key(seed)
    ks = jax.random.split(key, 24)
    f32 = jnp.float32

    def dense(k, shape, fan_in, gain=1.0):
        return jax.random.normal(k, shape, f32) * (gain * fan_in ** -0.5)

    def gain_vec(k, shape):
        return 1.0 + 0.02 * jax.random.normal(k, shape, f32)

    dt = jnp.exp(jax.random.uniform(ks[11], (DEPTH, 2, DN_HEADS), f32, math.log(1e-3), math.log(1e-1)))
    return {
        'x': jax.random.normal(ks[0], (BATCH, SEQ, D_MODEL), f32),
        'c': jax.random.normal(ks[1], (BATCH, D_MODEL), f32),
        'ctx': jax.random.normal(ks[2], (BATCH, CTX_LEN, D_MODEL), f32),
        'c_ctx': jax.random.normal(ks[3], (D_MODEL,), f32),
        'norm1': gain_vec(ks[4], (DEPTH, D_MODEL)),
        'norm2': gain_vec(ks[5], (DEPTH, D_MODEL)),
        'w_ada': dense(ks[6], (DEPTH, D_MODEL, 6 * D_MODEL), D_MODEL, 0.5),
        'b_ada': 0.02 * jax.random.normal(ks[7], (DEPTH, 6 * D_MODEL), f32),
        'w_in': dense(ks[8], (DEPTH, D_MODEL, D_IN), D_MODEL),
        'conv_w': dense(ks[9], (DEPTH, DN_CONV, 3 * DN_WIDTH), DN_CONV),
        'a_log': jnp.log(jax.random.uniform(ks[10], (DEPTH, 2, DN_HEADS), f32, 1.0, 16.0)),
        'dt_bias': dt + jnp.log(-jnp.expm1(-dt)),
        'dn_gain': gain_vec(ks[12], (DEPTH, DN_HEAD_DIM)),
        'lam_vecs': 0.1 * jax.random.normal(ks[13], (DEPTH, 4, DA_QK_DIM), f32),
        'da_gain': gain_vec(ks[14], (DEPTH, DA_V_DIM)),
        'w_f': dense(ks[15], (DEPTH, F_WIDTH, D_MODEL), F_WIDTH),
        'w_dn': dense(ks[16], (DEPTH, DN_WIDTH, D_MODEL), DN_WIDTH),
        'w_da': dense(ks[17], (DEPTH, DA_WIDTH, D_MODEL), DA_WIDTH),
        'w_o': dense(ks[18], (DEPTH, D_MODEL, D_MODEL), D_MODEL),
        'w_mlp1': dense(ks[19], (DEPTH, D_MODEL, D_FF), D_MODEL),
        'w_mlp2': dense(ks[20], (DEPTH, D_FF, D_MODEL), D_FF),
        'final_norm': gain_vec(ks[21], (D_MODEL,)),
    }


def reference(x, c, ctx, c_ctx, norm1, norm2, w_ada, b_ada, w_in, conv_w, a_log, dt_bias, dn_gain, lam_vecs,
              da_gain, w_f, w_dn, w_da, w_o, w_mlp1, w_mlp2, final_norm):
    ROWS = x.shape[1] // GRID_W
    tabs = axial_rope_tables(ROWS)
    c_act = jax.nn.silu(c)
    cc_act = jax.nn.silu(c_ctx)
    xc = ctx
    for l in range(DEPTH):
        need_ctx = l < DEPTH - 1
        lam_init = 0.8 - 0.6 * math.exp(-0.3 * l)
        sh1, sc1, g1, sh2, sc2, g2 = (m[:, None, :] for m in jnp.split(c_act @ w_ada[l] + b_ada[l], 6, axis=-1))
        sh1c, sc1c, g1c, sh2c, sc2c, g2c = jnp.split(cc_act @ w_ada[l] + b_ada[l], 6, axis=-1)
        h = rms_norm(x, norm1[l]) * (1 + sc1) + sh1
        hc = rms_norm(xc, norm1[l]) * (1 + sc1c) + sh1c
        y, yc = hybrid_mixer(h, hc, w_in[l], conv_w[l], a_log[l], dt_bias[l], dn_gain[l], lam_vecs[l], da_gain[l],
                             w_f[l], w_dn[l], w_da[l], w_o[l], tabs, lam_init, need_ctx)
        x = x + g1 * y
        h2 = rms_norm(x, norm2[l]) * (1 + sc2) + sh2
        x = x + g2 * squared_relu_mlp(h2, w_mlp1[l], w_mlp2[l])
        if need_ctx:
            xc = xc + g1c * yc
            h2c = rms_norm(xc, norm2[l]) * (1 + sc2c) + sh2c
            xc = xc + g2c * squared_relu_mlp(h2c, w_mlp1[l], w_mlp2[l])
    return rms_norm(x, final_norm)
```

```python
import numpy as np
from contextlib import ExitStack
import concourse.bass as bass
import concourse.mybir as mybir
from concourse.bass_utils import run_bass_kernel_spmd

F32 = mybir.dt.float32
BF16 = mybir.dt.bfloat16
I32 = mybir.dt.int32
AF = mybir.ActivationFunctionType
ALU = mybir.AluOpType
AX = mybir.AxisListType


class Buf:
    __slots__ = ("name", "w", "r", "excl")

    def __init__(self, name, excl=False):
        self.name = name
        self.w = None
        self.r = {}
        self.excl = excl


class View:
    __slots__ = ("ap", "bufs")

    def __init__(self, ap, bufs):
        self.ap = ap
        self.bufs = bufs


class Tile:
    def __init__(self, h, name, tracked=True, excl=False):
        self.h = h
        self.name = name
        self.base = Buf(name, excl) if tracked else None
        self.regs = {}
        self.excl = excl

    def __getitem__(self, idx):
        if self.base is None:
            return View(self.h[idx], [])
        return View(self.h[idx], [self.base] + list(self.regs.values()))

    def reg(self, key, idx):
        if self.excl:
            return View(self.h[idx], [self.base])
        b = self.regs.get(key)
        if b is None:
            b = self.regs[key] = Buf(f"{self.name}.{key}")
        return View(self.h[idx], [b])

    def v(self, ap, key=None):
        if self.base is None:
            return View(ap, [])
        if key is None:
            return View(ap, [self.base] + list(self.regs.values()))
        b = self.regs.get(key)
        if b is None:
            b = self.regs[key] = Buf(f"{self.name}.{key}")
        return View(ap, [b])


ENGS = ("pe", "act", "dve", "pool", "sp")
NDS = 8


class Prog:
    def __init__(self, nc, st):
        self.nc = nc
        self.st = st
        self.ops = {e: [] for e in ENGS}
        self.cnt = {e: 0 for e in ENGS}
        self.sem = {e: st.enter_context(nc.semaphore(f"s_{e}")) for e in ("pe", "act", "dve", "pool")}
        self.dsem = {q: [st.enter_context(nc.semaphore(f"d_{q}{i}")) for i in range(NDS)] for q in ("sp", "pool", "act")}
        self.dcnt = {q: [0] * NDS for q in self.dsem}
        self.dnext = {q: 0 for q in self.dsem}
        self.waited = {e: {} for e in ENGS}
        self.semobj = {}
        self.nuid = 0
        self.n_ins = 0

    def sb(self, name, shape, dt=F32):
        self.nuid += 1
        h = self.st.enter_context(self.nc.sbuf_tensor(f"{name}_{self.nuid}", list(shape), dt))
        return Tile(h, name)

    def ps(self, name, shape, dt=F32):
        self.nuid += 1
        h = self.st.enter_context(self.nc.psum_tensor(f"{name}_{self.nuid}", list(shape), dt))
        return Tile(h, name, excl=True)

    def dram(self, name, shape, dt, kind):
        h = self.nc.dram_tensor(name, list(shape), dt, kind=kind)
        return Tile(h.ap(), name, tracked=(kind != "ExternalInput"))

    def _deps(self, eng, reads, writes, extra=()):
        deps = {}
        def add(tok):
            if tok is None:
                return
            k = id(tok[0])
            self.semobj[k] = tok[0]
            if deps.get(k, 0) < tok[1]:
                deps[k] = tok[1]
        for v in reads:
            for b in v.bufs:
                add(b.w)
                if b.excl:
                    for t in b.r.values():
                        add(t)
        for v in writes:
            for b in v.bufs:
                add(b.w)
                for t in b.r.values():
                    add(t)
        for t in extra:
            add(t)
        waits = []
        own = id(self.sem[eng]) if eng in self.sem else None
        for k, val in deps.items():
            if eng == "pe" and k == own:
                continue
            if self.waited[eng].get(k, 0) < val:
                self.waited[eng][k] = val
                waits.append((self.semobj[k], val))
        return waits

    def _mark(self, tok, reads, writes):
        for v in writes:
            for b in v.bufs:
                b.w = tok
                b.r = {}
        for v in reads:
            for b in v.bufs:
                if b.excl:
                    b.w = tok
                    b.r = {}
                elif b.w is not tok:
                    k = id(tok[0])
                    if k not in b.r or b.r[k][1] < tok[1]:
                        b.r[k] = tok

    def op(self, eng, fn, reads=(), writes=()):
        reads = [v for v in reads if isinstance(v, View)]
        writes = [v for v in writes if isinstance(v, View)]
        waits = self._deps(eng, reads, writes)
        self.cnt[eng] += 1
        tok = (self.sem[eng], self.cnt[eng])
        self.ops[eng].append((waits, fn, (self.sem[eng], 1)))
        self._mark(tok, reads, writes)
        self.n_ins += 1 + len(waits)

    def dma(self, q, out, in_):
        j = self.dnext[q]
        self.dnext[q] = (j + 1) % NDS
        sem = self.dsem[q][j]
        prev = self.dcnt[q][j]
        extra = [(sem, 16 * prev)] if prev else []
        waits = self._deps(q, [in_], [out], extra)
        self.dcnt[q][j] += 1
        tok = (sem, 16 * (prev + 1))
        oa, ia = out.ap, in_.ap
        self.ops[q].append((waits, lambda e: e.dma_start(out=oa, in_=ia), (sem, 16)))
        self._mark(tok, [in_], [out])
        self.n_ins += 1 + len(waits)

    @staticmethod
    def _a(x):
        return x.ap if isinstance(x, View) else x

    def mm(self, out, lhsT, rhs, start=True, stop=True, sgc=False):
        o, l, r = out.ap, lhsT.ap, rhs.ap
        self.op("pe", lambda e: e.matmul(o, l, r, start=start, stop=stop, skip_group_check=sgc),
                reads=[lhsT, rhs] + ([] if start else [out]), writes=[out])

    def tr(self, out, in_, ident):
        o, i, d = out.ap, in_.ap, ident.ap
        self.op("pe", lambda e: e.transpose(o, i, d), reads=[in_, ident], writes=[out])

    def act(self, out, in_, func, bias=0.0, scale=1.0, accum=None):
        o, i, b, s = out.ap, in_.ap, self._a(bias), self._a(scale)
        ac = self._a(accum) if accum is not None else None
        if ac is None:
            f = lambda e: e.activation(o, i, func, bias=b, scale=s)
        else:
            f = lambda e: e.activation(o, i, func, bias=b, scale=s, accum_out=ac)
        self.op("act", f, reads=[in_, bias, scale], writes=[out] + ([accum] if accum is not None else []))

    def _ve(self, eng):
        return eng

    def tt(self, eng, out, a, b, op):
        o, x, y = out.ap, a.ap, b.ap
        self.op(eng, lambda e: e.tensor_tensor(o, x, y, op), reads=[a, b], writes=[out])

    def ts(self, eng, out, a, s1, op0, s2=None, op1=None, accum=None):
        o, x, p1, p2 = out.ap, a.ap, self._a(s1), self._a(s2)
        ac = self._a(accum) if accum is not None else None
        if op1 is None:
            f = lambda e: e.tensor_scalar(o, x, p1, None, op0)
        elif ac is None:
            f = lambda e: e.tensor_scalar(o, x, p1, p2, op0, op1)
        else:
            f = lambda e: e.tensor_scalar(o, x, p1, p2, op0, op1, accum_out=ac)
        self.op(eng, f, reads=[a, s1, s2], writes=[out] + ([accum] if accum is not None else []))

    def stt(self, eng, out, a, s, b, op0, op1):
        o, x, p, y = out.ap, a.ap, self._a(s), b.ap
        self.op(eng, lambda e: e.scalar_tensor_tensor(o, x, p, y, op0, op1), reads=[a, s, b], writes=[out])

    def copy(self, eng, out, in_):
        o, i = out.ap, in_.ap
        if eng == "act":
            self.op("act", lambda e: e.copy(o, i), reads=[in_], writes=[out])
        else:
            self.op(eng, lambda e: e.tensor_copy(o, i), reads=[in_], writes=[out])

    def memset(self, eng, out, val):
        o = out.ap
        self.op(eng, lambda e: e.memset(o, val), reads=[], writes=[out])

    def recip(self, out, in_):
        o, i = out.ap, in_.ap
        self.op("dve", lambda e: e.reciprocal(o, i), reads=[in_], writes=[out])

    def reduce(self, eng, out, in_, op, axis=AX.X):
        o, i = out.ap, in_.ap
        self.op(eng, lambda e: e.tensor_reduce(o, i, axis, op), reads=[in_], writes=[out])

    def iota(self, out, pattern, base=0, cm=0):
        o = out.ap
        self.op("pool", lambda e: e.iota(o, pattern, base=base, channel_multiplier=cm,
                                         allow_small_or_imprecise_dtypes=True), reads=[], writes=[out])

    def aselect(self, out, in_, pattern, cmp, fill, base=0, cm=0):
        o, i = out.ap, in_.ap
        self.op("pool", lambda e: e.affine_select(o, i, pattern, cmp, fill, base=base, channel_multiplier=cm),
                reads=[in_], writes=[out])

    def barrier(self):
        toks = [(self.sem[e], self.cnt[e]) for e in self.sem if self.cnt[e]]
        for q in self.dsem:
            for j in range(NDS):
                if self.dcnt[q][j]:
                    toks.append((self.dsem[q][j], 16 * self.dcnt[q][j]))
        for e in ENGS:
            waits = []
            for sem, val in toks:
                if self.waited[e].get(id(sem), 0) < val:
                    self.waited[e][id(sem)] = val
                    waits.append((sem, val))
            if waits:
                self.ops[e].append((waits, None, None))
                self.n_ins += len(waits)

    def finish(self):
        for q in self.dsem:
            for j in range(NDS):
                if self.dcnt[q][j]:
                    sem, val = self.dsem[q][j], 16 * self.dcnt[q][j]
                    if self.waited["sp"].get(id(sem), 0) < val:
                        self.waited["sp"][id(sem)] = val
                        self.ops["sp"].append(([(sem, val)], None, None))

    def emit(self):
        self.finish()
        nc = self.nc
        ops = self.ops

        def replay(e, lst):
            for waits, fn, inc in lst:
                for sem, val in waits:
                    e.wait_ge(sem, val)
                if fn is not None:
                    ins = fn(e)
                    ins.then_inc(inc[0], inc[1])

        with nc.Block() as block:
            @block.tensor
            def _(e):
                replay(e, ops["pe"])

            @block.scalar
            def _(e):
                replay(e, ops["act"])

            @block.vector
            def _(e):
                replay(e, ops["dve"])

            @block.gpsimd
            def _(e):
                replay(e, ops["pool"])

            @block.sync
            def _(e):
                replay(e, ops["sp"])


class Arena:
    def __init__(self, P, nbytes):
        self.P = P
        self.h = P.st.enter_context(P.nc.sbuf_tensor("arena", [128, nbytes // 4], F32))
        self.off = 0
        self.cap = nbytes
        self.n = 0

    def alloc(self, name, shape, dt=F32):
        esz = 2 if dt == BF16 else 4
        n = int(np.prod(shape[1:]))
        nb = (n * esz + 63) // 64 * 64
        assert self.off + nb <= self.cap, (name, self.off, nb, self.cap)
        ap = self.h[0:shape[0], self.off // 4:(self.off + nb) // 4]
        if dt != F32:
            ap = ap.bitcast(dt)
        ap = ap[:, 0:n]
        if len(shape) == 3:
            ap = ap.rearrange("p (a b) -> p a b", a=shape[1])
        elif len(shape) == 4:
            ap = ap.rearrange("p (a b c) -> p a b c", a=shape[1], b=shape[2])
        self.off += nb
        self.n += 1
        return Tile(ap, f"{name}{self.n}")

    def mark(self):
        return self.off

    def release(self, m=0):
        self.P.barrier()
        self.off = m


import math

TL, TCX = 8192, 256
TA = TL + TCX
NT = TA // 128
EPS = 1e-6

CO = {}
_o = 0
for _nm, _w in [("ident", 128), ("ones", 128), ("incl_f", 128), ("incl_b", 128), ("strict_f", 128), ("strict_b", 128),
                ("FC", 256), ("R2", 256), ("twc", 128), ("tws", 128), ("C64", 64), ("S64n", 64),
                ("CT", 512), ("STn", 512)]:
    CO[_nm] = (_o, _w)
    _o += _w
NCST = _o
SO = {"conv": (0, 15), "alog": (15, 2), "dtb": (17, 2), "dng": (19, 128), "dag": (147, 128), "lamv": (275, 256), "lami": (531, 2)}
NSM = 533


def host_consts():
    c = np.zeros((128, NCST), np.float32)
    i = np.arange(128)
    def put(nm, a):
        o, w = CO[nm]
        c[:a.shape[0], o:o + w] = a
    put("ident", np.eye(128))
    put("ones", np.ones((128, 128)))
    put("incl_f", (i[:, None] >= i[None, :]).astype(np.float32))
    put("incl_b", (i[:, None] <= i[None, :]).astype(np.float32))
    put("strict_f", (i[:, None] > i[None, :]).astype(np.float32))
    put("strict_b", (i[:, None] < i[None, :]).astype(np.float32))
    ang = 2 * np.pi * ((i[:, None] * i[None, :]) % 128) / 128
    C, S = np.cos(ang), np.sin(ang)
    put("FC", np.concatenate([C, S], 1))
    put("R2", np.concatenate([-S, C], 1))
    n2 = np.arange(64)
    angt = 2 * np.pi * ((n2[:, None] * i[None, :]) % 8192) / 8192
    put("twc", np.cos(angt)); put("tws", np.sin(angt))
    a64 = 2 * np.pi * ((n2[:, None] * n2[None, :]) % 64) / 64
    put("C64", np.cos(a64) / 1024.0); put("S64n", -np.sin(a64) / 1024.0)
    k = np.arange(256)
    ct = np.zeros((128, 2, 256)); st = np.zeros((128, 2, 256))
    for nt in range(2):
        a = 2 * np.pi * (((i + 128 * nt)[:, None] * k[None, :]) % 256) / 256
        ct[:, nt], st[:, nt] = np.cos(a), -np.sin(a)
    put("CT", ct.reshape(128, 512)); put("STn", st.reshape(128, 512))
    scale = 1.0
    c[:, CO["FC"][0]:CO["R2"][0] + 256] *= scale
    return c


def host_rope():
    t = np.arange(TL)
    row = (t // 64).astype(np.float32); col = (t % 64).astype(np.float32)
    inv = (10000.0 ** (-np.arange(16, dtype=np.float32) / 16)).astype(np.float32)
    cosT = np.zeros((128, TL), np.float32); sinT = np.zeros((128, TL), np.float32)
    for p in range(128):
        d = p % 64
        pos = row if d < 32 else col
        f = d % 16
        ang = pos * inv[f]
        cosT[p] = np.cos(ang)
        sinT[p] = np.sin(ang) * (-1.0 if (d % 32) < 16 else 1.0)
    return cosT, sinT


def rope_perm():
    idx = np.arange(128)
    d = idx % 64
    return np.where((d % 32) < 16, idx + 16, idx - 16)


class Slots:
    def __init__(self, ps, banks):
        self.ps = ps
        self.banks = banks
        self.i = 0

    def get(self, parts=128, cols=128):
        s = self.i
        nb = len(self.banks)
        self.i = (self.i + 1) % (nb * 4)
        b = self.ps[self.banks[s % nb]]
        q = s // nb
        return b.reg(q, (slice(0, parts), slice(q * 128, q * 128 + cols)))


def build_B(stages=("P", "F", "A", "D")):
    nc = bass.Bass("TRN2", target_bir_lowering=False)
    with ExitStack() as st:
        P = Prog(nc, st)
        A = Arena(P, 184 * 1024)
        ps = [P.ps(f"bank{i}", [128, 512]) for i in range(8)]
        xT = P.dram("xT", [1024, TA], F32, "ExternalInput")
        modin = P.dram("modin", [128, 5, 8], F32, "ExternalInput")
        wsel = P.dram("wsel", [1024, 1284], F32, "ExternalInput")
        ropec = P.dram("ropec", [128, TL], F32, "ExternalInput")
        ropes = P.dram("ropes", [128, TL], F32, "ExternalInput")
        smalls = P.dram("smalls", [128, NSM], F32, "ExternalInput")
        cstd = P.dram("cst", [128, NCST], F32, "ExternalInput")
        o_fm = P.dram("o_fm", [64, 16384], BF16, "ExternalOutput")
        o_fmc = P.dram("o_fmc", [256, 128], BF16, "ExternalOutput")
        o_dn = P.dram("o_dn", [TA, 128], BF16, "ExternalOutput")
        o_da = P.dram("o_da", [TA, 128], BF16, "ExternalOutput")
        s_u = P.dram("s_u", [128, TA], BF16, "Internal")
        s_dn = P.dram("s_dn", [3, 128, TA], F32, "Internal")
        s_dnp = P.dram("s_dnp", [3, 128, TA], F32, "Internal")
        s_q = P.dram("s_q", [128, TA], BF16, "Internal")
        s_k = P.dram("s_k", [128, TA], BF16, "Internal")
        s_v = P.dram("s_v", [TA, 136], BF16, "Internal")
        s_zab = P.dram("s_zab", [TA, 132], F32, "Internal")

        cst = P.sb("cst", [128, NCST])
        P.dma("sp", cst[:], cstd[:])
        sm = P.sb("sm", [128, NSM])
        P.dma("sp", sm[:], smalls[:])

        def C(nm, parts=128):
            o, w = CO[nm]
            return cst[0:parts, o:o + w]

        def S(nm, i=0, w=None):
            o, ww = SO[nm]
            return sm[:, o + i:o + i + (w or 1)]

        cb = P.sb("cstb", [128, 640], BF16)
        P.copy("dve", cb[:, 0:512], cst[:, CO["FC"][0]:CO["FC"][0] + 512])
        P.copy("dve", cb[:, 512:640], cst[:, CO["C64"][0]:CO["C64"][0] + 128])
        cb2 = P.sb("cstb2", [128, 1024], BF16)
        P.copy("pool", cb2[:], cst[:, CO["CT"][0]:CO["CT"][0] + 1024])
        FCb, R2b = cb[:, 0:256], cb[:, 256:512]
        C64b, S64nb = cb[0:64, 512:576], cb[0:64, 576:640]
        epst = P.sb("eps", [128, 1])
        P.memset("pool", epst[:], EPS)

        xTv = xT.h.rearrange("(k p) t -> p k t", p=128)
        wv = wsel.h.rearrange("(k p) c -> p k c", p=128)

        if "P" in stages:
            modt = A.alloc("modt", [128, 5, 8])
            P.dma("sp", modt[:], modin[:])
            a_l = A.alloc("a_l", [128, 8]); a_c = A.alloc("a_c", [128, 8])
            P.stt("dve", a_l[:], modt[:, 1, :], 1.0, modt[:, 0, :], ALU.add, ALU.mult)
            P.stt("dve", a_c[:], modt[:, 3, :], 1.0, modt[:, 0, :], ALU.add, ALU.mult)
            wb = A.alloc("wb", [128, 8, 1284], BF16)
            wst = [A.alloc("wst", [128, 8, 214]) for _ in range(2)]
            for i in range(6):
                s = wst[i % 2]
                P.dma("sp", s[:], View(wv[:, :, i * 214:(i + 1) * 214], []))
                P.copy(["dve", "pool"][i % 2], wb[:, :, i * 214:(i + 1) * 214], s[:])
            xb = [A.alloc("xb", [128, 8, 512]) for _ in range(2)]
            hb = [A.alloc("hb", [128, 8, 512], BF16) for _ in range(2)]
            sq = [A.alloc("sq", [128, 512]) for _ in range(2)]
            tmp = [A.alloc("tmp", [128, 512]) for _ in range(2)]
            rs = A.alloc("rs", [128, 512])
            cosb = [A.alloc("cosb", [128, 512]) for _ in range(2)]
            sinb = [A.alloc("sinb", [128, 512]) for _ in range(2)]
            ou = [A.alloc("ou", [128, 512], BF16) for _ in range(2)]
            odn = [A.alloc("odn", [128, 512]) for _ in range(3)]
            oq = [A.alloc("oq", [128, 512], BF16) for _ in range(2)]
            t1 = [A.alloc("t1", [128, 512]) for _ in range(2)]
            t2 = [A.alloc("t2", [128, 512]) for _ in range(2)]
            vst = [A.alloc("vst", [128, 4, 136], BF16) for _ in range(2)]
            zst = [A.alloc("zst", [128, 4, 132]) for _ in range(2)]
            for v in vst:
                P.memset("pool", v[:], 1.0)
            s_vv = s_v.h.rearrange("(t p) c -> p t c", p=128)
            s_zv = s_zab.h.rearrange("(t p) c -> p t c", p=128)
            chunks = [(0, 256, False)] + [(256 + i * 512, 512, True) for i in range(16)]
            import os
            DBG = int(os.environ.get("PB_DBG", "9"))
            if DBG < 9:
                chunks = chunks[:2]
            for ci, (t0, n, lat) in enumerate(chunks):
                if DBG < 2:
                    break
                xc = xb[ci % 2]; h = hb[ci % 2]
                P.dma("sp", xc[:, :, 0:n], View(xTv[:, :, t0:t0 + n], []))
                if lat:
                    P.dma("pool", cosb[ci % 2][:], ropec[:, t0 - 256:t0 - 256 + n])
                    P.dma("pool", sinb[ci % 2][:], ropes[:, t0 - 256:t0 - 256 + n])
                for k in range(8):
                    P.tt("pool", sq[k % 2][:, 0:n], xc[:, k, 0:n], xc[:, k, 0:n], ALU.mult)
                    P.mm(ps[7][:, 0:n], C("ones"), sq[k % 2][:, 0:n], start=(k == 0), stop=(k == 7))
                P.act(rs[:, 0:n], ps[7][:, 0:n], AF.Sqrt, bias=epst[:], scale=1.0 / 1024)
                P.recip(rs[:, 0:n], rs[:, 0:n])
                av, bi = (a_l, 2) if lat else (a_c, 4)
                for k in range(8):
                    P.tt("dve", tmp[k % 2][:, 0:n], xc[:, k, 0:n], rs[:, 0:n], ALU.mult)
                    P.act(h[:, k, 0:n], tmp[k % 2][:, 0:n], AF.Identity, bias=modt[:, bi, k:k + 1], scale=av[:, k:k + 1])
                for j in range(8):
                    if DBG < 3:
                        break
                    if DBG < 4 and j >= 4:
                        break
                    if not lat and j in (5, 7):
                        continue
                    pb = ps[j % 4]
                    for k in range(8):
                        P.mm(pb[:, 0:n], wb[:, k, j * 128:(j + 1) * 128], h[:, k, 0:n], start=(k == 0), stop=(k == 7))
                    if j == 0:
                        o = ou[ci % 2]
                        P.copy("act", o[:, 0:n], pb[:, 0:n])
                        P.dma("sp", s_u[:, t0:t0 + n], o[:, 0:n])
                    elif j <= 3:
                        o = odn[j - 1]
                        P.copy("act", o[:, 0:n], pb[:, 0:n])
                        P.dma("sp", s_dn.v(s_dn.h[j - 1, :, t0:t0 + n]), o[:, 0:n])
                    else:
                        qk = (j - 4) // 2
                        o = oq[qk]
                        dst = (s_q if qk == 0 else s_k)
                        if not lat:
                            P.copy("act", o[:, 0:n], pb[:, 0:n])
                            P.dma("sp", dst[:, t0:t0 + n], o[:, 0:n])
                        elif j % 2 == 0:
                            P.tt("dve", t1[qk][:, 0:n], pb[:, 0:n], cosb[ci % 2][:, 0:n], ALU.mult)
                        else:
                            P.tt("dve", t2[qk][:, 0:n], pb[:, 0:n], sinb[ci % 2][:, 0:n], ALU.mult)
                            P.tt("pool", o[:, 0:n], t1[qk][:, 0:n], t2[qk][:, 0:n], ALU.add)
                            P.dma("sp", dst[:, t0:t0 + n], o[:, 0:n])
                nt = n // 128
                vs, zs = vst[ci % 2], zst[ci % 2]
                if DBG < 5:
                    continue
                for ti in range(nt):
                    pb = ps[4 + ti % 2]
                    for k in range(8):
                        P.mm(pb[:, 0:260], h[:, k, ti * 128:(ti + 1) * 128], wb[:, k, 1024:1284], start=(k == 0), stop=(k == 7))
                    P.copy("act", vs[:, ti, 0:128], pb[:, 0:128])
                    P.copy("dve", zs[:, ti, :], pb[:, 128:260])
                tile0 = t0 // 128
                P.dma("pool", s_v.v(s_vv[:, tile0:tile0 + nt, :]), vs[:, 0:nt, :])
                P.dma("pool", s_zab.v(s_zv[:, tile0:tile0 + nt, :]), zs[:, 0:nt, :])
            A.release()

        if "F" in stages:
            u = A.alloc("u", [128, TA], BF16)
            P.dma("sp", u[:, 0:4224], s_u[:, 0:4224])
            P.dma("pool", u[:, 4224:TA], s_u[:, 4224:TA])
            Zc = A.alloc("Zc", [128, 64, 256], BF16)
            for n2 in range(64):
                pb = ps[n2 % 4]
                P.mm(pb[:, 0:256], u.v(u.h[:, 256 + n2:TA:64]), FCb)
                P.copy(["act", "dve"][n2 % 2], Zc[:, n2, :], pb[:, 0:256])
            Gre = A.alloc("Gre", [64, 128 * 128], BF16)
            Gim = A.alloc("Gim", [64, 128 * 128], BF16)
            Gs = [A.alloc("Gs", [64, 8, 256]) for _ in range(2)]
            ft = [A.alloc("ft", [64, 8, 128]) for _ in range(4)]
            twc = cst.h[0:64, CO["twc"][0]:CO["twc"][0] + 128].unsqueeze(1).to_broadcast([64, 8, 128])
            tws = cst.h[0:64, CO["tws"][0]:CO["tws"][0] + 128].unsqueeze(1).to_broadcast([64, 8, 128])
            twc, tws = cst.v(twc), cst.v(tws)
            for c in range(128):
                pb = ps[4 + c % 4]
                g = Gs[(c // 8) % 2]
                P.mm(pb[0:64, 0:256], Zc.v(Zc.h[:, :, c]), FCb, start=True, stop=False)
                P.mm(pb[0:64, 0:256], Zc.v(Zc.h[:, :, 128 + c]), R2b, start=False, stop=True)
                P.copy("act", g[:, c % 8, :], pb[0:64, 0:256])
                if c % 8 == 7:
                    c0 = c - 7
                    re, im = g.v(g.h[:, :, 0:128]), g.v(g.h[:, :, 128:256])
                    gre = Gre.v(Gre.h[:, c0 * 128:(c0 + 8) * 128].rearrange("p (a b) -> p a b", a=8))
                    gim = Gim.v(Gim.h[:, c0 * 128:(c0 + 8) * 128].rearrange("p (a b) -> p a b", a=8))
                    P.tt("dve", ft[0][:], re, twc, ALU.mult)
                    P.tt("dve", ft[1][:], im, tws, ALU.mult)
                    P.tt("dve", gre, ft[0][:], ft[1][:], ALU.subtract)
                    P.tt("pool", ft[2][:], im, twc, ALU.mult)
                    P.tt("pool", ft[3][:], re, tws, ALU.mult)
                    P.tt("pool", gim, ft[2][:], ft[3][:], ALU.add)
            ost = [A.alloc("ost", [64, 512], BF16) for _ in range(2)]
            for ch in range(32):
                pb = ps[ch % 4]
                P.mm(pb[0:64, :], C64b, Gre[:, ch * 512:(ch + 1) * 512], start=True, stop=False)
                P.mm(pb[0:64, :], S64nb, Gim[:, ch * 512:(ch + 1) * 512], start=False, stop=True)
                P.copy(["act", "dve"][ch % 2], ost[ch % 2][:], pb[0:64, :])
                P.dma("sp", o_fm[:, ch * 512:(ch + 1) * 512], ost[ch % 2][:])
            Zx = A.alloc("Zx", [128, 2, 256], BF16)
            for nt in range(2):
                pb = ps[4 + nt]
                P.mm(pb[:, 0:256], u[:, nt * 128:(nt + 1) * 128], FCb)
                P.copy("act", Zx[:, nt, :], pb[:, 0:256])
            oc = [A.alloc("oc", [128, 128], BF16) for _ in range(2)]
            for kt in range(2):
                pb = ps[6 + kt]
                for nt in range(2):
                    P.mm(pb[:, 0:128], cb2[:, nt * 256 + kt * 128: nt * 256 + (kt + 1) * 128], Zx[:, nt, 0:128], start=(nt == 0), stop=False)
                    P.mm(pb[:, 0:128], cb2[:, 512 + nt * 256 + kt * 128: 512 + nt * 256 + (kt + 1) * 128], Zx[:, nt, 128:256], start=False, stop=(nt == 1))
                P.act(oc[kt][:], pb[:, 0:128], AF.Copy, scale=float(1.0 / np.sqrt(256.0 * 128.0)))
                P.dma("sp", o_fmc[kt * 128:(kt + 1) * 128, :], oc[kt][:])
            A.release()

        if "A" in stages:
            kT = A.alloc("kT", [128, TA], BF16)
            P.dma("sp", kT[:, 0:4224], s_k[:, 0:4224])
            P.dma("pool", kT[:, 4224:TA], s_k[:, 4224:TA])
            va = A.alloc("va", [128, NT, 136], BF16)
            s_vv = s_v.h.rearrange("(t p) c -> p t c", p=128)
            P.dma("sp", va[:, 0:33, :], s_v.v(s_vv[:, 0:33, :]))
            P.dma("pool", va[:, 33:NT, :], s_v.v(s_vv[:, 33:NT, :]))
            lt = A.alloc("lt", [128, 64]); l1 = A.alloc("l1", [128, 1]); l2 = A.alloc("l2", [128, 1])
            nlam = A.alloc("nlam", [128, 1])
            P.tt("dve", lt[:], S("lamv", 0, 64), S("lamv", 64, 64), ALU.mult)
            P.reduce("dve", l1[:], lt[:], ALU.add)
            P.tt("dve", lt[:], S("lamv", 128, 64), S("lamv", 192, 64), ALU.mult)
            P.reduce("dve", l2[:], lt[:], ALU.add)
            P.act(l1[:], l1[:], AF.Exp)
            P.act(l2[:], l2[:], AF.Exp)
            P.tt("dve", nlam[:], l2[:], l1[:], ALU.subtract)
            P.tt("dve", nlam[:], nlam[:], S("lami", 0), ALU.subtract)
            gl = A.alloc("gl", [128, 128])
            P.ts("dve", gl[:], S("dag", 0, 128), S("lami", 1), ALU.mult)
            qb = [A.alloc("qb", [128, 512], BF16) for _ in range(2)]
            pt = [A.alloc("pt", [128, 512], BF16) for _ in range(4)]
            o0 = [A.alloc("o0", [128, 128]) for _ in range(2)]
            oo = [A.alloc("oo", [128, 128]) for _ in range(2)]
            osq = A.alloc("osq", [128, 128])
            rz = [A.alloc("rz", [128, 4]) for _ in range(2)]
            ob = [A.alloc("ob", [128, 128], BF16) for _ in range(2)]

            def acc(m, qs):
                a = m * 4 + qs
                return ps[4 + a // 2].reg(a % 2, (slice(0, 128), slice((a % 2) * 256, (a % 2) * 256 + 129)))

            qchunks = [(0, 256, range(0, 2))] + [(256 + i * 512, 512, range(0, NT)) for i in range(16)]
            cnt = 0
            for qi, (t0, nq, kts) in enumerate(qchunks):
                q = qb[qi % 2]
                P.dma("sp", q[:, 0:nq], s_q[:, t0:t0 + nq])
                nqs = nq // 128
                for bk in range(4, 8):
                    P.memset("dve", ps[bk][:], 0.0)
                for kt in kts:
                    for m in range(2):
                        pb = ps[cnt % 4]; pp = pt[cnt % 4]; cnt += 1
                        P.mm(pb[:, 0:nq], kT[m * 64:(m + 1) * 64, kt * 128:(kt + 1) * 128], q[m * 64:(m + 1) * 64, 0:nq])
                        P.act(pp[:, 0:nq], pb[:, 0:nq], AF.Exp, scale=0.125)
                        for qs in range(nqs):
                            P.mm(acc(m, qs), pp[:, qs * 128:(qs + 1) * 128], va[:, kt, 0:129], start=False, stop=(kt == kts[-1]), sgc=True)
                for qs in range(nqs):
                    r = rz[qs % 2]; a0, a1 = acc(0, qs), acc(1, qs)
                    a0v = View(a0.ap[:, 0:128], a0.bufs); a1v = View(a1.ap[:, 0:128], a1.bufs)
                    P.recip(r[:, 0:1], View(a0.ap[:, 128:129], a0.bufs))
                    P.recip(r[:, 1:2], View(a1.ap[:, 128:129], a1.bufs))
                    P.tt("dve", r[:, 2:3], r[:, 1:2], nlam[:], ALU.mult)
                    P.ts("dve", o0[qs % 2][:], a0v, r[:, 0:1], ALU.mult)
                    P.stt("dve", oo[qs % 2][:], a1v, r[:, 2:3], o0[qs % 2][:], ALU.mult, ALU.add)
                    P.act(osq[:], oo[qs % 2][:], AF.Square, accum=r[:, 3:4])
                    P.act(r[:, 3:4], r[:, 3:4], AF.Sqrt, bias=epst[:], scale=1.0 / 128)
                    P.recip(r[:, 3:4], r[:, 3:4])
                    P.stt("dve", ob[qs % 2][:], oo[qs % 2][:], r[:, 3:4], gl[:], ALU.mult, ALU.mult)
                    P.dma("pool", o_da[t0 + qs * 128:t0 + (qs + 1) * 128, :], ob[qs % 2][:])
            A.release()

        if "D" in stages:
            stage_D(P, A, ps, dict(s_dn=s_dn, s_dnp=s_dnp, s_zab=s_zab, o_dn=o_dn, C=C, S=S, epst=epst, cst=cst))
        P.emit()
    return nc, P


TL, TCX = 8192, 256
TA = TL + TCX
NT = TA // 128


def stage_D(P, A, ps, E):
    s_dn, s_dnp, s_zab, o_dn, C, S, epst = E["s_dn"], E["s_dnp"], E["s_zab"], E["o_dn"], E["C"], E["S"], E["epst"]
    xp = [A.alloc("xp", [128, 516]) for _ in range(2)]
    ca = [A.alloc("ca", [128, 512]) for _ in range(2)]
    cy = [A.alloc("cy", [128, 512]) for _ in range(2)]
    csq = [A.alloc("csq", [128, 512]) for _ in range(2)]
    crs = [A.alloc("crs", [128, 512]) for _ in range(2)]
    it = 0
    for j in range(3):
        for (s0, L) in ((0, TCX), (TCX, TL)):
            for c0 in range(0, L, 512):
                n = min(512, L - c0)
                x = xp[it % 2]; acc = ca[it % 2]; y = cy[it % 2]; sq = csq[it % 2]; rs = crs[it % 2]
                lo, hi = max(c0 - 2, 0), min(c0 + n + 2, L)
                if c0 == 0:
                    P.memset("pool", x[:, 0:2], 0.0)
                if c0 + n == L:
                    P.memset("pool", x[:, n + 2:n + 4], 0.0)
                P.dma("sp", x[:, lo - (c0 - 2):hi - (c0 - 2)], s_dn.v(s_dn.h[j, :, s0 + lo:s0 + hi]))
                eng = "dve"
                P.ts(eng, acc[:, 0:n], x[:, 0:n], S("conv", j * 5 + 0), ALU.mult)
                for tap in range(1, 5):
                    P.stt(eng, acc[:, 0:n], x[:, tap:tap + n], S("conv", j * 5 + tap), acc[:, 0:n], ALU.mult, ALU.add)
                P.act(y[:, 0:n], acc[:, 0:n], AF.Silu)
                if j < 2:
                    P.tt("pool", sq[:, 0:n], y[:, 0:n], y[:, 0:n], ALU.mult)
                    pb = ps[it % 4]
                    P.mm(pb[:, 0:n], C("ones"), sq[:, 0:n])
                    P.act(rs[:, 0:n], pb[:, 0:n], AF.Sqrt, bias=epst[:], scale=1.0)
                    P.recip(rs[:, 0:n], rs[:, 0:n])
                    if j == 0:
                        P.stt(eng, y[:, 0:n], y[:, 0:n], 128 ** -0.5, rs[:, 0:n], ALU.mult, ALU.mult)
                    else:
                        P.tt(eng, y[:, 0:n], y[:, 0:n], rs[:, 0:n], ALU.mult)
                P.dma("pool", s_dnp.v(s_dnp.h[j, :, s0 + c0:s0 + c0 + n]), y[:, 0:n])
                it += 1
    A.release()

    zab = A.alloc("zab", [128, NT, 132])
    s_zv = s_zab.h.rearrange("(t p) c -> p t c", p=128)
    P.dma("sp", zab[:, 0:33, :], s_zab.v(s_zv[:, 0:33, :]))
    P.dma("pool", zab[:, 33:NT, :], s_zab.v(s_zv[:, 33:NT, :]))
    beta = A.alloc("beta", [128, NT, 2])
    gg = A.alloc("gg", [128, NT, 2])
    nA = A.alloc("nA", [128, 2])
    P.act(nA[:], S("alog", 0, 2), AF.Exp)
    P.ts("dve", nA[:], nA[:], -1.0, ALU.mult)
    P.act(beta[:], zab.v(zab.h[:, :, 128:130]), AF.Sigmoid)
    for d in range(2):
        gv = gg.v(gg.h[:, :, d])
        P.act(gv, zab.v(zab.h[:, :, 130 + d]), AF.Exp, bias=S("dtb", d), scale=1.0)
        P.act(gv, gv, AF.Ln, bias=1.0, scale=1.0)
        P.ts("dve", gv, gv, nA[:, d:d + 1], ALU.mult)
    zv = zab.v(zab.h[:, :, 0:128])
    P.act(zv, zv, AF.Silu)

    oall = A.alloc("oall", [128, NT, 128])
    Sst = [[A.alloc("S", [128, 128]) for _ in range(2)] for _ in range(2)]
    for d in range(2):
        P.memset("pool", Sst[d][0][:], 0.0)
    NB = 3

    def ring(name, shape, n=NB):
        return [A.alloc(name, shape) for _ in range(n)]

    R = {}
    for d in range(2):
        R[d] = {nm: ring(nm, [128, 128]) for nm in
                ("Ktm", "Vtm", "gB", "ng", "Ee", "Dm", "Ds", "L", "LT", "QKD", "QKDT", "Kbg", "Ke", "bV", "WT", "U", "Vn", "o2", "oo")}
        R[d]["qkv"] = ring("qkv", [128, 3, 128])
        R[d]["sc"] = ring("sc", [128, 8])
        R[d]["Pa"] = ring("Pa", [128, 128], 4)
        R[d]["PaT"] = ring("PaT", [128, 128], 4)
        R[d]["Y"] = ring("Y", [128, 128], 4)
    fin = {nm: ring(nm, [128, 128], 2) for nm in ("fsq", "ft")}
    fsc = ring("fsc", [128, 2], 2)
    fob = [A.alloc("fob", [128, 128], BF16) for _ in range(2)]
    slots = Slots(ps, list(range(8)))
    ident = C("ident")
    ones = C("ones")
    order = {0: [0, 1] + list(range(2, NT)), 1: [1, 0] + list(range(NT - 1, 1, -1))}
    pos = {d: {n: i for i, n in enumerate(order[d])} for d in range(2)}
    cnt = {0: 0, 1: 0}
    pre = {}

    def evac(eng, dst, src):
        P.copy(eng, dst, src)

    def precompute(d, n):
        i = cnt[d] % NB
        r = {k: (v[i] if len(v) == NB else v) for k, v in R[d].items()}
        qkv = r["qkv"]
        P.dma("sp", qkv[:], s_dnp.v(s_dnp.h[:, :, n * 128:(n + 1) * 128].rearrange("j p t -> p j t")))
        QT, KT, VT = qkv[:, 0, :], qkv[:, 1, :], qkv[:, 2, :]
        incl = C("incl_f") if d == 0 else C("incl_b")
        strict = C("strict_f") if d == 0 else C("strict_b")
        tri = C("incl_b") if d == 0 else C("incl_f")
        sc = r["sc"]
        gcol = gg.v(gg.h[:, n, d:d + 1]); bcol = beta.v(beta.h[:, n, d:d + 1])
        p1 = slots.get(); P.tr(p1, KT, ident); evac("act", r["Ktm"][:], p1)
        p2 = slots.get(); P.tr(p2, VT, ident); evac("act", r["Vtm"][:], p2)
        p3 = slots.get(128, 1); P.mm(p3, tri, gcol); evac("dve", sc[:, 0:1], p3)
        p4 = slots.get(128, 1); P.mm(p4, ones, gcol); evac("dve", sc[:, 1:2], p4)
        P.act(sc[:, 2:3], sc[:, 1:2], AF.Exp)
        P.act(sc[:, 3:4], sc[:, 0:1], AF.Exp)
        P.tt("dve", sc[:, 4:5], sc[:, 3:4], bcol, ALU.mult)
        P.act(sc[:, 5:6], sc[:, 0:1], AF.Exp, bias=sc[:, 1:2], scale=-1.0)
        P.ts("dve", r["gB"][:], ones, gcol, ALU.mult)
        p5 = slots.get(); P.mm(p5, r["gB"][:], tri)
        P.ts("dve", r["ng"][:], p5, sc[:, 0:1], ALU.subtract, 0.0, ALU.max)
        P.act(r["Ee"][:], r["ng"][:], AF.Exp, scale=-1.0)
        P.tt("pool", r["Dm"][:], r["Ee"][:], incl, ALU.mult)
        P.tt("pool", r["Ds"][:], r["Ee"][:], strict, ALU.mult)
        p6 = slots.get(); P.mm(p6, KT, KT)
        P.stt("dve", r["L"][:], p6, bcol, r["Ds"][:], ALU.mult, ALU.mult)
        p7 = slots.get(); P.tr(p7, r["L"][:], ident); evac("act", r["LT"][:], p7)
        p8 = slots.get(); P.mm(p8, QT, KT)
        P.tt("dve", r["QKD"][:], p8, r["Dm"][:], ALU.mult)
        p9 = slots.get(); P.tr(p9, r["QKD"][:], ident); evac("act", r["QKDT"][:], p9)
        P.ts("pool", r["Kbg"][:], r["Ktm"][:], sc[:, 4:5], ALU.mult)
        P.ts("pool", r["Ke"][:], r["Ktm"][:], sc[:, 5:6], ALU.mult)
        P.ts("pool", r["bV"][:], r["Vtm"][:], bcol, ALU.mult)
        Pa, PaT, Y = R[d]["Pa"], R[d]["PaT"], R[d]["Y"]
        y = Y[0]
        P.tt("dve", y[:], ident, r["LT"][:], ALU.subtract)
        pk, pkT = r["L"], r["LT"]
        for lvl in range(1, 7):
            a = slots.get(); P.mm(a, pkT[:], pk[:])
            npk = Pa[lvl % 4]; evac("act", npk[:], a)
            if lvl < 6:
                b = slots.get(); P.mm(b, pk[:], pkT[:])
                npkT = PaT[lvl % 4]; evac("dve", npkT[:], b)
            c = slots.get(); P.mm(c, npk[:], y[:])
            ny = Y[lvl % 4]
            P.tt("dve", ny[:], y[:], c, ALU.add)
            y = ny
            pk = npk
            if lvl < 6:
                pkT = npkT
        pw = slots.get(); P.mm(pw, r["Kbg"][:], y[:]); evac("act", r["WT"][:], pw)
        pu = slots.get(); P.mm(pu, y[:], r["bV"][:]); evac("dve", r["U"][:], pu)
        pre[(d, n)] = (r, QT)
        cnt[d] += 1

    def scan(d, n, k):
        r, QT = pre.pop((d, n))
        sc = r["sc"]
        Sc, Sn = Sst[d][k % 2], Sst[d][(k + 1) % 2]
        a = slots.get(); P.mm(a, r["WT"][:], Sc[:])
        P.tt("dve", r["Vn"][:], r["U"][:], a, ALU.subtract)
        b = slots.get(); P.mm(b, QT, Sc[:])
        c = slots.get(); P.mm(c, r["QKDT"][:], r["Vn"][:])
        e = slots.get(); P.mm(e, r["Ke"][:], r["Vn"][:])
        P.stt("dve", Sn[:], Sc[:], sc[:, 2:3], e, ALU.mult, ALU.add)
        P.copy("act", r["o2"][:], c)
        first = pos[d][n] <= pos[1 - d][n] if d == 0 else pos[d][n] < pos[1 - d][n]
        ov = oall.reg(n, (slice(None), n, slice(None)))
        if first:
            P.stt("dve", ov, b, sc[:, 3:4], r["o2"][:], ALU.mult, ALU.add)
        else:
            P.stt("dve", r["oo"][:], b, sc[:, 3:4], r["o2"][:], ALU.mult, ALU.add)
            P.tt("pool", ov, ov, r["oo"][:], ALU.add)
            f = n % 2
            P.act(fin["fsq"][f][:], ov, AF.Square, accum=fsc[f][:, 0:1])
            P.act(fsc[f][:, 1:2], fsc[f][:, 0:1], AF.Sqrt, bias=epst[:], scale=1.0 / 128)
            P.recip(fsc[f][:, 1:2], fsc[f][:, 1:2])
            P.stt("dve", fin["ft"][f][:], ov, fsc[f][:, 1:2], S("dng", 0, 128), ALU.mult, ALU.mult)
            P.tt("pool", fob[f][:], fin["ft"][f][:], zab.v(zab.h[:, n, 0:128]), ALU.mult)
            P.dma("pool", o_dn[n * 128:(n + 1) * 128, :], fob[f][:])

    for d in range(2):
        precompute(d, order[d][0])
    for k in range(NT):
        if k + 1 < NT:
            for d in range(2):
                precompute(d, order[d][k + 1])
        for d in range(2):
            scan(d, order[d][k], k)
    A.release()


NTC = 2112
EPS = 1e-6


def build_M():
    nc = bass.Bass("TRN2", target_bir_lowering=False)
    with ExitStack() as st:
        P = Prog(nc, st)
        wa = P.dram("wa", [1024, 1536], F32, "ExternalInput")
        ba = P.dram("ba", [128, 12], F32, "ExternalInput")
        cv = P.dram("cv", [128, 8, 3], F32, "ExternalInput")
        out = P.dram("mod", [128, 36], F32, "ExternalOutput")
        wat = P.sb("wat", [128, 8, 1536])
        wav = wa.h.rearrange("(k p) c -> p k c", p=128)
        for k in range(8):
            P.dma(["sp", "pool"][k % 2], wat[:, k, :], View(wav[:, k, :], []))
        cvt = P.sb("cvt", [128, 8, 3]); cat = P.sb("cat", [128, 8, 3]); bat = P.sb("bat", [128, 12])
        P.dma("sp", cvt[:], cv[:]); P.dma("sp", bat[:], ba[:])
        P.act(cat[:], cvt[:], AF.Silu)
        pb = P.ps("pb", [128, 512])
        ot = P.sb("ot", [128, 36])
        for cc in range(12):
            for k in range(8):
                P.mm(pb[:, cc * 3:(cc + 1) * 3], wat[:, k, cc * 128:(cc + 1) * 128], cat[:, k, :], start=(k == 0), stop=(k == 7))
        for cc in range(12):
            P.ts("dve", ot[:, cc * 3:(cc + 1) * 3], pb[:, cc * 3:(cc + 1) * 3], bat[:, cc:cc + 1], ALU.add)
        P.dma("sp", out[:], ot[:])
        P.emit()
    return nc, P


def build_C():
    nc = bass.Bass("TRN2", target_bir_lowering=False)
    with ExitStack() as st:
        P = Prog(nc, st)
        xT = P.dram("xT", [1024, NTC], F32, "ExternalInput")
        brT = [P.dram(nm, [512, NTC], BF16, "ExternalInput") for nm in ("fmT", "dnT", "daT")]
        modl = P.dram("modl", [128, 6, 8], F32, "ExternalInput")
        modc = P.dram("modc", [128, 6, 8], F32, "ExternalInput")
        gains = P.dram("gains", [128, 3, 8], F32, "ExternalInput")
        wg = P.dram("wg", [1024, 3072], F32, "ExternalInput")
        wbr = [P.dram(nm, [512, 1024], F32, "ExternalInput") for nm in ("wf", "wdn", "wda")]
        wo = P.dram("wo", [1024, 1024], F32, "ExternalInput")
        w1 = P.dram("w1", [1024, 4096], F32, "ExternalInput")
        w2 = P.dram("w2", [4096, 1024], F32, "ExternalInput")
        xoT = P.dram("xoT", [1024, NTC], F32, "ExternalOutput")
        xnT = P.dram("xnT", [1024, NTC], F32, "ExternalOutput")
        s_x1 = P.dram("s_x1", [1024, NTC], F32, "Internal")
        xTv = xT.h.rearrange("(k p) t -> p k t", p=128)
        x1v = s_x1.h.rearrange("(k p) t -> p k t", p=128)
        xov = xoT.h.rearrange("(k p) t -> p k t", p=128)
        xnv = xnT.h.rearrange("(k p) t -> p k t", p=128)

        WA = P.sb("WA", [128, 65536], BF16)
        ones = P.sb("ones", [128, 128]); P.memset("pool", ones[:], 1.0)
        epst = P.sb("eps", [128, 1]); P.memset("pool", epst[:], EPS)
        ml = P.sb("ml", [128, 6, 8]); mc = P.sb("mc", [128, 6, 8]); gn = P.sb("gn", [128, 3, 8])
        P.dma("sp", ml[:], modl[:]); P.dma("sp", mc[:], modc[:]); P.dma("sp", gn[:], gains[:])
        av = P.sb("av", [128, 4, 8])
        for i, (m, scix, gix) in enumerate(((ml, 1, 0), (mc, 1, 0), (ml, 4, 1), (mc, 4, 1))):
            P.stt("dve", av[:, i, :], m[:, scix, :], 1.0, gn[:, gix, :], ALU.add, ALU.mult)
        ps = [P.ps(f"bank{i}", [128, 512]) for i in range(8)]
        A = Arena(P, 76 * 1024)
        stg = [A.alloc("stg", [128, 2048]) for _ in range(3)]
        nld = [0]

        def load_w(dst_off, src_ap2d, ncols):
            c = 0
            while c < ncols:
                w = min(2048, ncols - c)
                s = stg[nld[0] % 3]
                P.dma(["sp", "pool"][nld[0] % 2], s[:, 0:w], View(src_ap2d[:, c:c + w], []))
                P.copy(["dve", "pool", "act"][nld[0] % 3], WA[:, dst_off + c:dst_off + c + w], s[:, 0:w])
                nld[0] += 1
                c += w

        chunks = [(0, 64, False)] + [(64 + i * 256, 256, True) for i in range(8)]

        def norm(xc, n, a_col, b_col, h):
            for k in range(8):
                P.tt("pool", sq[k % 2][:, 0:n], xc[:, k, 0:n], xc[:, k, 0:n], ALU.mult)
                P.mm(ps[7][:, 0:n], ones[:], sq[k % 2][:, 0:n], start=(k == 0), stop=(k == 7))
            P.act(rs[:, 0:n], ps[7][:, 0:n], AF.Sqrt, bias=epst[:], scale=1.0 / 1024)
            P.recip(rs[:, 0:n], rs[:, 0:n])
            for k in range(8):
                P.tt("dve", tmp[k % 2][:, 0:n], xc[:, k, 0:n], rs[:, 0:n], ALU.mult)
                if b_col is None:
                    P.ts("pool", h[:, k, 0:n], tmp[k % 2][:, 0:n], a_col(k), ALU.mult)
                else:
                    P.act(h[:, k, 0:n], tmp[k % 2][:, 0:n], AF.Identity, bias=b_col(k), scale=a_col(k))

        OFF_G, OFF_BR, OFF_O = 0, 8 * 3072, 8 * 3072 + 12 * 1024
        wgv = wg.h.rearrange("(k p) c -> p k c", p=128)
        for k in range(8):
            load_w(OFF_G + k * 3072, wgv[:, k, :], 3072)
        for br in range(3):
            v = wbr[br].h.rearrange("(k p) c -> p k c", p=128)
            for k in range(4):
                load_w(OFF_BR + (br * 4 + k) * 1024, v[:, k, :], 1024)
        wov = wo.h.rearrange("(k p) c -> p k c", p=128)
        for k in range(8):
            load_w(OFF_O + k * 1024, wov[:, k, :], 1024)
        m0 = A.mark()
        xc = A.alloc("xc", [128, 8, 256]); h = A.alloc("h", [128, 8, 256], BF16)
        sq = [A.alloc("sq", [128, 256]) for _ in range(2)]; tmp = [A.alloc("tmp", [128, 256]) for _ in range(2)]
        rs = A.alloc("rs", [128, 256])
        acts = [A.alloc("acts", [128, 4, 256], BF16) for _ in range(3)]
        mg = A.alloc("mg", [128, 8, 256], BF16)
        sg = [A.alloc("sg", [128, 256]) for _ in range(2)]
        macc = A.alloc("macc", [128, 256]); mt = [A.alloc("mt", [128, 256]) for _ in range(2)]
        x1 = A.alloc("x1", [128, 8, 256])
        it = 0
        for (t0, n, lat) in chunks:
            m = ml if lat else mc
            ai = 0 if lat else 1
            P.dma("sp", xc[:, :, 0:n], View(xTv[:, :, t0:t0 + n], []))
            for br in range(3):
                P.dma("pool", acts[br][:, :, 0:n], View(brT[br].h.rearrange("(h p) t -> p h t", p=128)[:, :, t0:t0 + n], []))
            norm(xc, n, lambda k: av[:, ai, k:k + 1], lambda k: m[:, 0, k:k + 1], h)
            for oc in range(8):
                for br in range(3):
                    pg = ps[it % 2]; py = ps[2 + it % 2]; s = sg[it % 2]; it += 1
                    col = br * 1024 + oc * 128
                    for k in range(8):
                        P.mm(pg[:, 0:n], WA[:, OFF_G + k * 3072 + col:OFF_G + k * 3072 + col + 128], h[:, k, 0:n], start=(k == 0), stop=(k == 7))
                    P.act(s[:, 0:n], pg[:, 0:n], AF.Sigmoid)
                    for k in range(4):
                        o = OFF_BR + (br * 4 + k) * 1024 + oc * 128
                        P.mm(py[:, 0:n], WA[:, o:o + 128], acts[br][:, k, 0:n], start=(k == 0), stop=(k == 3))
                    if br == 0:
                        P.tt("dve", macc[:, 0:n], s[:, 0:n], py[:, 0:n], ALU.mult)
                    elif br == 1:
                        P.tt("dve", mt[0][:, 0:n], s[:, 0:n], py[:, 0:n], ALU.mult)
                        P.tt("pool", macc[:, 0:n], macc[:, 0:n], mt[0][:, 0:n], ALU.add)
                    else:
                        P.tt("dve", mt[1][:, 0:n], s[:, 0:n], py[:, 0:n], ALU.mult)
                        P.tt("pool", mg[:, oc, 0:n], macc[:, 0:n], mt[1][:, 0:n], ALU.add)
            for oc in range(8):
                po = ps[4 + oc % 2]
                for k in range(8):
                    o = OFF_O + k * 1024 + oc * 128
                    P.mm(po[:, 0:n], WA[:, o:o + 128], mg[:, k, 0:n], start=(k == 0), stop=(k == 7))
                P.stt("dve", x1[:, oc, 0:n], po[:, 0:n], m[:, 2, oc:oc + 1], xc[:, oc, 0:n], ALU.mult, ALU.add)
            P.dma("sp", s_x1.v(x1v[:, :, t0:t0 + n]), x1[:, :, 0:n])
        A.release(m0)

        OFF_1, OFF_2 = 0, 8 * 4096
        w1v = w1.h.rearrange("(k p) c -> p k c", p=128)
        for k in range(8):
            load_w(OFF_1 + k * 4096, w1v[:, k, :], 4096)
        w2v = w2.h.rearrange("(k p) c -> p k c", p=128)
        for k in range(32):
            load_w(OFF_2 + k * 1024, w2v[:, k, :], 1024)
        xc = A.alloc("xc", [128, 8, 256]); h = A.alloc("h", [128, 8, 256], BF16)
        sq = [A.alloc("sq", [128, 256]) for _ in range(2)]; tmp = [A.alloc("tmp", [128, 256]) for _ in range(2)]
        rs = A.alloc("rs", [128, 256])
        hid = A.alloc("hid", [128, 32, 256], BF16)
        ar = [A.alloc("ar", [128, 256]) for _ in range(2)]
        x2 = A.alloc("x2", [128, 8, 256]); xn = A.alloc("xn", [128, 8, 256])
        for (t0, n, lat) in chunks:
            m = ml if lat else mc
            ai = 2 if lat else 3
            P.dma("sp", xc[:, :, 0:n], s_x1.v(x1v[:, :, t0:t0 + n]))
            norm(xc, n, lambda k: av[:, ai, k:k + 1], lambda k: m[:, 3, k:k + 1], h)
            for j in range(32):
                ph = ps[j % 2]
                for k in range(8):
                    o = OFF_1 + k * 4096 + j * 128
                    P.mm(ph[:, 0:n], WA[:, o:o + 128], h[:, k, 0:n], start=(k == 0), stop=(k == 7))
                P.act(ar[j % 2][:, 0:n], ph[:, 0:n], AF.Relu)
                P.tt("pool", hid[:, j, 0:n], ar[j % 2][:, 0:n], ar[j % 2][:, 0:n], ALU.mult)
            for oc in range(8):
                py = ps[2 + oc % 4]
                for j in range(32):
                    o = OFF_2 + j * 1024 + oc * 128
                    P.mm(py[:, 0:n], WA[:, o:o + 128], hid[:, j, 0:n], start=(j == 0), stop=(j == 31))
                P.stt("dve", x2[:, oc, 0:n], py[:, 0:n], m[:, 5, oc:oc + 1], xc[:, oc, 0:n], ALU.mult, ALU.add)
            P.dma("sp", xoT.v(xov[:, :, t0:t0 + n]), x2[:, :, 0:n])
            norm(x2, n, lambda k: gn[:, 2, k:k + 1], None, xn)
            P.dma("pool", xnT.v(xnv[:, :, t0:t0 + n]), xn[:, :, 0:n])
        P.emit()
    return nc, P

import numpy as np, math


def split_mod(m):
    return [m[i * 1024:(i + 1) * 1024] for i in range(6)]

def fm8(v):
    return np.ascontiguousarray(v.reshape(8, 128).T)

def b_inputs(l, b, hd, xT_b, mod_lat, mod_ctx, inp, consts, ropes):
    sh1, sc1 = split_mod(mod_lat)[0:2]
    sh1c, sc1c = split_mod(mod_ctx)[0:2]
    modin = np.stack([fm8(inp['norm1'][l]), fm8(sc1), fm8(sh1), fm8(sc1c), fm8(sh1c)], axis=1).astype(np.float32)
    w = inp['w_in'][l]
    H = hd * 128
    perm = rope_perm()
    def cols(o): return w[:, o + H:o + H + 128]
    ab_cols = w[:, [2560 + j * 4 + hd for j in range(4)]]
    daq, dak = cols(2576), cols(3088)
    wsel = np.concatenate([cols(0), cols(512), cols(1024), cols(1536), daq, daq[:, perm], dak, dak[:, perm],
                           cols(3600), cols(2048), ab_cols], axis=1)
    sm = np.zeros((128, NSM), np.float32)
    cw = inp['conv_w'][l]
    for j in range(3):
        sm[:, j * 5:(j + 1) * 5] = cw[:, j * 512 + H:j * 512 + H + 128].T
    sm[:, 15:17] = inp['a_log'][l][:, hd][None, :]
    sm[:, 17:19] = inp['dt_bias'][l][:, hd][None, :]
    sm[:, 19:147] = inp['dn_gain'][l][None, :]
    sm[:, 147:275] = inp['da_gain'][l][None, :]
    sm[:, 275:531] = inp['lam_vecs'][l].reshape(1, 256)
    lam_init = 0.8 - 0.6 * math.exp(-0.3 * l)
    sm[:, 531] = lam_init; sm[:, 532] = 1.0 - lam_init
    return {"xT": np.ascontiguousarray(xT_b), "modin": np.ascontiguousarray(modin), "wsel": np.ascontiguousarray(wsel),
            "ropec": ropes[0], "ropes": ropes[1], "smalls": sm, "cst": consts}

def b_unpack(res):
    fm = res["o_fm"].reshape(64, 128, 128)
    fm_lat = np.transpose(fm, (0, 2, 1)).reshape(8192, 128)
    return fm_lat, res["o_fmc"], res["o_dn"], res["o_da"]


_CACHE = {}


def _prog(name, builder):
    if name not in _CACHE:
        _CACHE[name] = builder()[0]
    return _CACHE[name]


def _run(nc, in_maps):
    return run_bass_kernel_spmd(nc, in_maps, core_ids=list(range(8))).results


def _fm68(v):
    return np.ascontiguousarray(v.reshape(6, 8, 128).transpose(2, 0, 1))


def kernel(**inputs):
    inp = {k: np.asarray(v) for k, v in inputs.items()}
    x = inp['x'].astype(np.float32)
    ctx = inp['ctx'].astype(np.float32)
    ncM = _prog('M', build_M)
    cvecs = np.stack([inp['c'][0], inp['c'][1], inp['c_ctx']], axis=1).astype(np.float32)
    cv = np.ascontiguousarray(cvecs.reshape(8, 128, 3).transpose(1, 0, 2))
    in_maps = []
    for core in range(8):
        l, j = core // 4, core % 4
        wa = np.ascontiguousarray(inp['w_ada'][l][:, j * 1536:(j + 1) * 1536])
        ba = np.ascontiguousarray(inp['b_ada'][l][j * 1536:(j + 1) * 1536].reshape(12, 128).T)
        in_maps.append({"wa": wa, "ba": ba, "cv": cv})
    res = _run(ncM, in_maps)
    mod = np.zeros((2, 3, 6144), np.float32)
    for core in range(8):
        l, j = core // 4, core % 4
        o = np.asarray(res[core]["mod"]).reshape(128, 12, 3)
        mod[l, :, j * 1536:(j + 1) * 1536] = o.transpose(2, 1, 0).reshape(3, 1536)
    consts = host_consts()
    ropes = host_rope()
    xs = [x[0].copy(), x[1].copy()]
    xcs = [ctx[0].copy(), ctx[1].copy()]
    ncB = _prog('B', build_B)
    ncC = _prog('C', build_C)
    out = np.zeros((2, 8192, 1024), np.float32)
    for l in range(2):
        in_maps = []
        for core in range(8):
            b, hd = core // 4, core % 4
            xT_b = np.concatenate([xcs[b].T, xs[b].T], axis=1)
            in_maps.append(b_inputs(l, b, hd, xT_b, mod[l, b], mod[l, 2], inp, consts, ropes))
        resB = _run(ncB, in_maps)
        outs = [b_unpack(r) for r in resB]
        gains = np.ascontiguousarray(np.stack([fm8(inp['norm1'][l]), fm8(inp['norm2'][l]), fm8(inp['final_norm'])], axis=1).astype(np.float32))
        wgc = np.ascontiguousarray(inp['w_in'][l][:, 4112:7184])
        in_maps = []
        for core in range(8):
            b, q = core // 4, core % 4
            ls = slice(q * 2048, (q + 1) * 2048)
            cs = slice(q * 64, (q + 1) * 64)
            xT = np.ascontiguousarray(np.concatenate([xcs[b][cs].T, xs[b][ls].T], axis=1))
            fmT = np.concatenate([np.concatenate([np.asarray(outs[b * 4 + hd][1])[cs].T, np.asarray(outs[b * 4 + hd][0])[ls].T], axis=1) for hd in range(4)], axis=0)
            dnT = np.concatenate([np.concatenate([np.asarray(outs[b * 4 + hd][2])[:256][cs].T, np.asarray(outs[b * 4 + hd][2])[256:][ls].T], axis=1) for hd in range(4)], axis=0)
            daT = np.concatenate([np.concatenate([np.asarray(outs[b * 4 + hd][3])[:256][cs].T, np.asarray(outs[b * 4 + hd][3])[256:][ls].T], axis=1) for hd in range(4)], axis=0)
            in_maps.append({"xT": xT, "fmT": np.ascontiguousarray(fmT), "dnT": np.ascontiguousarray(dnT), "daT": np.ascontiguousarray(daT),
                            "modl": _fm68(mod[l, b]), "modc": _fm68(mod[l, 2]), "gains": gains, "wg": wgc,
                            "wf": np.ascontiguousarray(inp['w_f'][l]), "wdn": np.ascontiguousarray(inp['w_dn'][l]),
                            "wda": np.ascontiguousarray(inp['w_da'][l]), "wo": np.ascontiguousarray(inp['w_o'][l]),
                            "w1": np.ascontiguousarray(inp['w_mlp1'][l]), "w2": np.ascontiguousarray(inp['w_mlp2'][l])})
        resC = _run(ncC, in_maps)
        for core in range(8):
            b, q = core // 4, core % 4
            ls = slice(q * 2048, (q + 1) * 2048)
            cs = slice(q * 64, (q + 1) * 64)
            xo = np.asarray(resC[core]["xoT"])
            xcs[b][cs] = xo[:, :64].T
            xs[b][ls] = xo[:, 64:].T
            if l == 1:
                out[b, ls] = np.asarray(resC[core]["xnT"])[:, 64:].T
    return out
```
